# Optimizing a Trainium2 kernel written in Bass

```python
import math
import jax, jax.numpy as jnp
from jax import lax
import numpy as np

D_MODEL = 2048
BATCH = 1
SEQ = 16384
DEPTH = 2

HEAD_DIM = 128
BRANCH_HEADS = 4
BRANCH_WIDTH = BRANCH_HEADS * HEAD_DIM
N_BRANCH = 4
A_KV_HEADS = 2
A_HALF_WINDOW = 128
B_WIN_ROWS = 8
B_WIN_COLS = 16
C_PATTERNS = ((128, 1), (512, 4), (2048, 16))
C_GROUPS = len(C_PATTERNS)
C_KV_HEADS = BRANCH_HEADS
D_KV_HEADS = 2
GRID_W = 64
ROPE_THETA = 10000.0
Q_BLOCK = 128
NORM_EPS = 1e-6
F32 = jnp.float32

IN_SPLITS = (
    BRANCH_WIDTH, A_KV_HEADS * HEAD_DIM, A_KV_HEADS * HEAD_DIM, BRANCH_WIDTH,
    BRANCH_WIDTH, BRANCH_WIDTH, BRANCH_WIDTH, BRANCH_WIDTH,
    C_GROUPS * BRANCH_WIDTH, C_KV_HEADS * HEAD_DIM, C_KV_HEADS * HEAD_DIM, BRANCH_WIDTH,
    BRANCH_WIDTH, D_KV_HEADS * HEAD_DIM, D_KV_HEADS * HEAD_DIM, BRANCH_WIDTH,
    N_BRANCH * D_MODEL,
)
IN_WIDTH = sum(IN_SPLITS)
IN_OFFSETS = tuple(int(o) for o in np.cumsum(IN_SPLITS)[:-1])

kernel_name = 'hybrid_gated_four_mixer_encoder'


def rms_norm(x, g):
    xf = x.astype(F32)
    y = xf * lax.rsqrt(jnp.mean(xf * xf, axis=-1, keepdims=True) + NORM_EPS)
    return (y * g.astype(F32)).astype(x.dtype)


def rope_angles(pos, dim):
    inv_freq = ROPE_THETA ** (-jnp.arange(dim // 2, dtype=F32) / (dim // 2))
    return pos.astype(F32)[:, None] * inv_freq[None, :]


def apply_rope(x, ang):
    half = x.shape[-1] // 2
    cos = jnp.cos(ang)[None, :, None, :]
    sin = jnp.sin(ang)[None, :, None, :]
    xf = x.astype(F32)
    x1, x2 = xf[..., :half], xf[..., half:]
    return jnp.concatenate([x1 * cos - x2 * sin, x2 * cos + x1 * sin], axis=-1).astype(x.dtype)


def axial_rope(x, ang_row, ang_col):
    half = x.shape[-1] // 2
    return jnp.concatenate([apply_rope(x[..., :half], ang_row), apply_rope(x[..., half:], ang_col)], axis=-1)


def banded_attention(q, k, v, half_window, block, sink=None):
    n, length, hk, g, dh = q.shape
    nb = length // block
    span = block + 2 * half_window
    qb = q.reshape(n, nb, block, hk, g, dh)
    pad = ((0, 0), (half_window, half_window), (0, 0), (0, 0))
    kidx = (jnp.arange(nb) * block)[:, None] + jnp.arange(span)[None, :]
    kb = jnp.pad(k, pad)[:, kidx]
    vb = jnp.pad(v, pad)[:, kidx]
    qpos = jnp.arange(length).reshape(nb, block)
    kpos = (kidx - half_window)[:, None, :]
    valid = (jnp.abs(kpos - qpos[:, :, None]) <= half_window) & (kpos >= 0) & (kpos < length)
    s = jnp.einsum('nbqhgd,nbkhd->nbhgqk', qb, kb, preferred_element_type=F32)
    s = jnp.where(valid[None, :, None, None], s, -jnp.inf)
    m = jnp.max(s, axis=-1, keepdims=True)
    if sink is not None:
        sink_b = sink.astype(F32)[None, None, :, :, None, None]
        m = jnp.maximum(m, sink_b)
    p = jnp.exp(s - m)
    denom = jnp.sum(p, axis=-1, keepdims=True)
    if sink is not None:
        denom = denom + jnp.exp(sink_b - m)
    o = jnp.einsum('nbhgqk,nbkhd->nbqhgd', p, vb.astype(F32)) / jnp.moveaxis(denom, 4, 2)
    lse = jnp.moveaxis((m + jnp.log(denom))[..., 0], 4, 2)
    return o.reshape(n, length, hk, g, dh).astype(q.dtype), lse.reshape(n, length, hk, g)


def to_strided(t, dil):
    b, s = t.shape[:2]
    rest = t.shape[2:]
    return jnp.moveaxis(t.reshape(b, s // dil, dil, *rest), 2, 1).reshape(b * dil, s // dil, *rest)


def from_strided(t, dil, b):
    nl = t.shape[1]
    rest = t.shape[2:]
    return jnp.moveaxis(t.reshape(b, dil, nl, *rest), 1, 2).reshape(b, dil * nl, *rest)


def dilated_mixture_attention(q, k, v):
    b, s = q.shape[:2]
    outs, lses = [], []
    for gi, (window, dil) in enumerate(C_PATTERNS):
        sub_len = s // dil
        half = window // (2 * dil)
        block = math.gcd(sub_len, Q_BLOCK)
        o, lse = banded_attention(to_strided(q[:, :, gi], dil)[:, :, :, None], to_strided(k, dil),
                                  to_strided(v, dil), half, block)
        outs.append(from_strided(o[:, :, :, 0], dil, b))
        lses.append(from_strided(lse[:, :, :, 0], dil, b))
    w = jax.nn.softmax(jnp.stack(lses, axis=0), axis=0)
    return jnp.einsum('gbsh,gbshd->bshd', w, jnp.stack(outs, axis=0).astype(F32)).astype(q.dtype)


def neighbourhood_attention(q, k, v, rpb):
    b, s, h, dh = q.shape
    rows = s // GRID_W
    kr = min(B_WIN_ROWS, rows)
    qg = q.reshape(b, rows, GRID_W, h, dh)
    kg = k.reshape(b, rows, GRID_W, h, dh)
    vg = v.reshape(b, rows, GRID_W, h, dh)
    r = jnp.arange(rows)
    row_start = jnp.clip(r - kr // 2, 0, rows - kr)
    ridx = row_start[:, None] + jnp.arange(kr)[None, :]
    kb = kg[:, ridx]
    vb = vg[:, ridx]
    s_ = jnp.einsum('brchd,brjkhd->brhcjk', qg, kb, preferred_element_type=F32)
    c = jnp.arange(GRID_W)
    col_start = jnp.clip(c - B_WIN_COLS // 2, 0, GRID_W - B_WIN_COLS)
    col_ok = (c[None, :] >= col_start[:, None]) & (c[None, :] < col_start[:, None] + B_WIN_COLS)
    dr = ridx - r[:, None] + (B_WIN_ROWS - 1)
    dc = jnp.clip(c[None, :] - c[:, None] + (B_WIN_COLS - 1), 0, 2 * B_WIN_COLS - 2)
    bias = rpb[:, dr[:, None, :, None], dc[None, :, None, :]]
    s_ = s_ + jnp.moveaxis(bias, 0, 1)[None].astype(F32)
    s_ = jnp.where(col_ok[:, None, :], s_, -jnp.inf)
    p = jax.nn.softmax(s_.reshape(b, rows, h, GRID_W, kr * GRID_W), axis=-1).reshape(s_.shape)
    o = jnp.einsum('brhcjk,brjkhd->brchd', p, vb.astype(F32))
    return o.reshape(b, s, h, dh).astype(q.dtype)


def dense_blocked_attention(q, k, v):
    b, s, hk, g, dh = q.shape
    nb = s // Q_BLOCK
    qb = jnp.moveaxis(q.reshape(b, nb, Q_BLOCK, hk, g, dh), 1, 0)
    vf = v.astype(F32)

    def one_block(qi):
        sc = jnp.einsum('bqhgd,bkhd->bhgqk', qi, k, preferred_element_type=F32)
        p = jax.nn.softmax(sc, axis=-1)
        return jnp.einsum('bhgqk,bkhd->bqhgd', p, vf).astype(q.dtype)

    o = lax.map(one_block, qb)
    return jnp.moveaxis(o, 0, 1).reshape(b, s, hk * g, dh)


def hybrid_layer(x, norm_g, w_in, gate_b, a_sink, b_rpb, d_q_norm, d_k_norm, w_branch, w_out,
                 ang_1d, ang_row, ang_col):
    b, s, _ = x.shape
    scale = HEAD_DIM ** -0.5
    h = rms_norm(x, norm_g)
    u = jnp.einsum('bsd,de->bse', h, w_in)
    (aq, ak, av, az, bq, bk, bv, bz, cq, ck, cv, cz, dq, dk, dv, dz, gates) = jnp.split(u, IN_OFFSETS, axis=-1)

    def heads(t):
        return t.reshape(b, s, t.shape[-1] // HEAD_DIM, HEAD_DIM)

    qa = (apply_rope(heads(aq), ang_1d) * scale).reshape(b, s, A_KV_HEADS, BRANCH_HEADS // A_KV_HEADS, HEAD_DIM)
    ya, _ = banded_attention(qa, apply_rope(heads(ak), ang_1d), heads(av), A_HALF_WINDOW, Q_BLOCK,
                             sink=a_sink.reshape(A_KV_HEADS, BRANCH_HEADS // A_KV_HEADS))
    ya = ya.reshape(b, s, BRANCH_WIDTH)
    yb = neighbourhood_attention(heads(bq) * scale, heads(bk), heads(bv), b_rpb).reshape(b, s, BRANCH_WIDTH)
    qc = (apply_rope(heads(cq), ang_1d) * scale).reshape(b, s, C_GROUPS, C_KV_HEADS, HEAD_DIM)
    yc = dilated_mixture_attention(qc, apply_rope(heads(ck), ang_1d), heads(cv)).reshape(b, s, BRANCH_WIDTH)
    qd = axial_rope(rms_norm(heads(dq), d_q_norm), ang_row, ang_col) * scale
    kd = axial_rope(rms_norm(heads(dk), d_k_norm), ang_row, ang_col)
    qd = qd.reshape(b, s, D_KV_HEADS, BRANCH_HEADS // D_KV_HEADS, HEAD_DIM)
    yd = dense_blocked_attention(qd, kd, heads(dv)).reshape(b, s, BRANCH_WIDTH)

    y = jnp.stack([ya, yb, yc, yd], axis=2)
    z = jnp.stack([az, bz, cz, dz], axis=2)
    branch = jnp.einsum('bsnm,nmd->bsnd', y * jax.nn.silu(z), w_branch)
    gate = jax.nn.sigmoid(gates.reshape(b, s, N_BRANCH, D_MODEL) + gate_b)
    merged = jnp.sum(gate * branch, axis=2)
    return x + jnp.einsum('bsd,de->bse', merged, w_out)


def setup_inputs(seed: int = 0) -> dict:
    key = jax.random.key(seed)
    ks = jax.random.split(key, 11)
    nrm = jax.random.normal
    return {
        'x': nrm(ks[0], (BATCH, SEQ, D_MODEL), F32),
        'norm_g': 1.0 + 0.02 * nrm(ks[1], (DEPTH, D_MODEL), F32),
        'w_in': nrm(ks[2], (DEPTH, D_MODEL, IN_WIDTH), F32) * D_MODEL ** -0.5,
        'gate_b': 0.1 * nrm(ks[3], (DEPTH, N_BRANCH, D_MODEL), F32),
        'a_sink': 0.5 * nrm(ks[4], (DEPTH, BRANCH_HEADS), F32),
        'b_rpb': 0.1 * nrm(ks[5], (DEPTH, BRANCH_HEADS, 2 * B_WIN_ROWS - 1, 2 * B_WIN_COLS - 1), F32),
        'd_q_norm': 1.0 + 0.02 * nrm(ks[6], (DEPTH, HEAD_DIM), F32),
        'd_k_norm': 1.0 + 0.02 * nrm(ks[7], (DEPTH, HEAD_DIM), F32),
        'w_branch': nrm(ks[8], (DEPTH, N_BRANCH, BRANCH_WIDTH, D_MODEL), F32) * BRANCH_WIDTH ** -0.5,
        'w_out': nrm(ks[9], (DEPTH, D_MODEL, D_MODEL), F32) * D_MODEL ** -0.5,
        'final_norm_g': 1.0 + 0.02 * nrm(ks[10], (D_MODEL,), F32),
    }


def reference(x, norm_g, w_in, gate_b, a_sink, b_rpb, d_q_norm, d_k_norm, w_branch, w_out, final_norm_g):
    s = x.shape[1]
    t = jnp.arange(s)
    ang_1d = rope_angles(t, HEAD_DIM)
    ang_row = rope_angles(t // GRID_W, HEAD_DIM // 2)
    ang_col = rope_angles(t % GRID_W, HEAD_DIM // 2)
    for layer in range(DEPTH):
        x = hybrid_layer(x, norm_g[layer], w_in[layer], gate_b[layer], a_sink[layer], b_rpb[layer],
                         d_q_norm[layer], d_k_norm[layer], w_branch[layer], w_out[layer],
                         ang_1d, ang_row, ang_col)
    return rms_norm(x, final_norm_g)
```

```python
import numpy as np
import ml_dtypes
from contextlib import ExitStack

import concourse.bass as bass
import concourse.mybir as mybir
from concourse.bass_utils import run_bass_kernel_spmd

F32 = mybir.dt.float32
BF16 = mybir.dt.bfloat16
AF = mybir.ActivationFunctionType
ALU = mybir.AluOpType
NPBF = ml_dtypes.bfloat16

NCORE = 8
S = 16384
D = 2048
T = S // NCORE
KC = 16
INW = 16384
GRID_W = 64
SCALE = 128 ** -0.5
EPS = 1e-6
KVROWS = 3072
NEG = -200.0

ENGS = ("tensor", "vector", "scalar", "gpsimd", "sync")


class Sem:
    __slots__ = ("h", "n")

    def __init__(self, h):
        self.h = h
        self.n = 0


class Res:
    __slots__ = ("w", "r", "sem", "name", "excl")

    def __init__(self, name="", excl=False):
        self.w = {}
        self.r = {}
        self.sem = None
        self.name = name
        self.excl = excl


class Ctx:
    def __init__(self, nc, stack):
        self.nc = nc
        self.stack = stack
        self.esem = {e: self.newsem("p_" + e) for e in ENGS}
        self.waited = {e: {} for e in ENGS}
        self.nsem = 0

    def newsem(self, name):
        return Sem(self.stack.enter_context(self.nc.semaphore(name)))


class Phase:
    def __init__(self, ctx, name):
        self.ctx = ctx
        self.name = name
        self.q = {e: [] for e in ENGS}
        self.dsems = []

    def _waits(self, eng, reads, writes, pwrites):
        need = {}

        def merge(d):
            for k, v in d.items():
                if need.get(k, 0) < v:
                    need[k] = v

        for r in reads:
            merge(r.w)
            if r.excl:
                merge(r.r)
        for w in writes:
            merge(w.w)
            merge(w.r)
        for w in pwrites:
            merge(w.r)
        out = []
        wd = self.ctx.waited[eng]
        for k, v in need.items():
            if wd.get(k, 0) < v:
                wd[k] = v
                out.append((k, v))
        return out

    @staticmethod
    def _commit(ev, reads, writes, pwrites):
        k, v = ev
        for r in reads:
            if r.r.get(k, 0) < v:
                r.r[k] = v
        for w in writes:
            w.w = {k: v}
            w.r = {}
        for w in pwrites:
            if w.w.get(k, 0) < v:
                w.w[k] = v

    def op(self, eng, fns, reads=(), writes=(), pwrites=()):
        if not isinstance(fns, (list, tuple)):
            fns = [fns]
        waits = self._waits(eng, reads, writes, pwrites)
        sem = self.ctx.esem[eng]
        sem.n += 1
        self.q[eng].append((waits, fns, sem, 1))
        self._commit((sem, sem.n), reads, writes, pwrites)

    def dma(self, eng, out, in_, owner, reads=(), writes=(), pwrites=()):
        if owner.sem is None:
            owner.sem = {}
        if eng not in owner.sem:
            self.ctx.nsem += 1
            owner.sem[eng] = self.ctx.newsem("d%d" % self.ctx.nsem)
        sem = owner.sem[eng]
        if sem not in self.dsems:
            self.dsems.append(sem)
        waits = self._waits(eng, reads, writes, pwrites)
        sem.n += 16
        if callable(in_):
            fn = lambda e, o=out, i=in_: e.dma_start(out=o, in_=i(e))
        else:
            fn = lambda e, o=out, i=in_: e.dma_start(out=o, in_=i)
        self.q[eng].append((waits, [fn], sem, 16))
        self._commit((sem, sem.n), reads, writes, pwrites)

    def emit(self):
        ctx = self.ctx
        fin = []
        wd = ctx.waited["sync"]
        for sem in self.dsems + [ctx.esem[e] for e in ENGS if e != "sync"]:
            if wd.get(sem, 0) < sem.n:
                wd[sem] = sem.n
                fin.append((sem, sem.n))
        self.q["sync"].append((fin, [], None, 0))
        with ctx.nc.Block() as block:
            for eng in ENGS:
                def body(e, eng=eng):
                    for waits, fns, sem, inc in self.q[eng]:
                        for (k, v) in waits:
                            e.wait_ge(k.h, v)
                        if not fns:
                            continue
                        for f in fns[:-1]:
                            f(e)
                        fns[-1](e).then_inc(sem.h, inc)
                getattr(block, eng)(body)


A_Q, A_K, A_V, A_Z = 0, 512, 768, 1024
B_Q, B_K, B_V, B_Z = 1536, 2048, 2560, 3072
C_Q, C_K, C_V, C_Z = 3584, 5120, 5632, 6144
D_Q, D_K, D_V, D_Z = 6656, 7168, 7424, 7680
G_0 = 8192

KV_JOBS = [
    (A_K, 256, "rope_k", 0),
    (C_K, 512, "rope_k", 6),
    (D_K, 256, "dnorm_k", 10),
    (B_K, 512, "plain_k", 2),
    (A_V, 256, "v", 0),
    (B_V, 512, "v", 2),
    (C_V, 512, "v", 6),
    (D_V, 256, "v", 10),
]
MAIN_JOBS = (
    [(A_Q, 512, "rope_q", 0), (C_Q, 512, "rope_q", 8), (C_Q + 512, 512, "rope_q", 12),
     (C_Q + 1024, 512, "rope_q", 16), (D_Q, 512, "dnorm_q", 20), (B_Q, 512, "scale_q", 4),
     (A_Z, 512, "silu", 0), (B_Z, 512, "silu", 4), (C_Z, 512, "silu", 8), (D_Z, 512, "silu", 12)]
    + [(G_0 + 512 * b, 512, "gate", 4 * b) for b in range(16)]
)

KV_RANGES = [(512, 1024), (2048, 3072), (5120, 6144), (7168, 7680)]
MAIN_RANGES = [(0, 512), (1024, 2048), (3072, 5120), (6144, 7168), (7680, 16384)]


def _colmap(ranges):
    def f(c):
        off = 0
        for a, b in ranges:
            if a <= c < b:
                return off + c - a
            off += b - a
        raise ValueError(c)
    return f, sum(b - a for a, b in ranges)


BRANCHES = [
    dict(name="A", n=0, kvslots=[0, 1], qbase=0, hq_per_kv=2, halo=128, sink=True, bias=False,
         groups=[(0, 6, -128)], qstride=1),
    dict(name="B", n=1, kvslots=[2, 3, 4, 5], qbase=4, hq_per_kv=1, halo=256, sink=False, bias=True,
         groups=[(None, 8, -256)], qstride=1),
    dict(name="C", n=2, kvslots=[6, 7, 8, 9], qbase=8, hq_per_kv=1, halo=1024, sink=False, bias=False,
         groups=[(6, 6, -128), (12, 8, -256), (20, 20, -1024)], qstride=4),
]
BVAR = {0: 0, 1: 1, 2: 1, 3: 2}


class Builder:
    def __init__(self, mode, nlayers, last_flags):
        self.mode = mode
        self.nl = nlayers
        self.last = last_flags
        self.nc = bass.Bass("TRN2", target_bir_lowering=False)

    def declare(self):
        nc, nl, mode = self.nc, self.nl, self.mode
        di = lambda name, shape, dt: nc.dram_tensor(name, shape, dt, kind="ExternalInput").ap()
        do = lambda name, shape, dt: nc.dram_tensor(name, shape, dt, kind="ExternalOutput").ap()
        dn = lambda name, shape, dt: nc.dram_tensor(name, shape, dt).ap()
        self.x = di("x", [T, D], F32)
        if mode == "kv":
            self.colmap, inw = _colmap(KV_RANGES)
        elif mode == "main":
            self.colmap, inw = _colmap(MAIN_RANGES)
        else:
            self.colmap, inw = (lambda c: c), INW
        self.w_in = di("w_in", [nl, D, inw], F32)
        self.norm_g = di("norm_g", [nl, D], F32)
        self.dqk = di("dqk", [128, 2 * nl], F32)
        self.ropeT = di("ropeT", [8, 128, T], F32)
        self.cmat = di("cmat", [4, 128, 128], BF16)
        if mode in ("main", "fused"):
            self.w_br = di("w_br", [nl, D, D], F32)
            self.w_out = di("w_out", [nl, D, D], F32)
            self.gbT = di("gbT", [nl, 128, 64], F32)
            self.asink = di("asink", [1, 4 * nl], F32)
            self.fng = di("fng", [1, D], F32)
            self.maskAC = di("maskAC", [4, 40, 128, 512], BF16)
            self.biasB = di("biasB", [nl, 3, 4, 8, 128, 512], F32)
            self.qT_s = dn("qT_s", [24, 128, T], BF16)
            self.zs_s = dn("zs_s", [16, 128, T], BF16)
            self.gs_s = dn("gs_s", [64, 128, T], BF16)
            self.ysz_s = dn("ysz_s", [16, 128, T], BF16)
            self.out = do("out", [T, D], F32)
            self.kvL_K = dn("kvL_K", [1280, T], BF16)
            self.kvL_V = dn("kvL_V", [1280, T], BF16)
            self.kvR_K = dn("kvR_K", [1280, T], BF16)
            self.kvR_V = dn("kvR_V", [1280, T], BF16)
        if mode == "kv":
            self.kv_loc = do("kv_loc", [KVROWS, T], BF16)
        elif mode == "main":
            self.kv_all = di("kv_all", [NCORE, KVROWS, T], BF16)
            self.kv_loc = di("kv_loc", [KVROWS, T], BF16)
        else:
            self.kv_locs = [dn("kv_loc%d" % i, [KVROWS, T], BF16) for i in range(nl)]
            self.kv_alls = [dn("kv_all%d" % i, [NCORE * KVROWS, T], BF16) for i in range(nl)]
            self.xmid = dn("xmid", [T, D], F32)

    def build(self):
        self.declare()
        nc = self.nc
        with ExitStack() as gs:
            self.gs = gs
            ctx = self.ctx = Ctx(nc, gs)
            sb = lambda name, shape, dt: gs.enter_context(nc.sbuf_tensor(name, shape, dt))
            self.banks = [gs.enter_context(nc.psum_tensor("bank%d" % i, [128, 512], F32)) for i in range(8)]
            self.Rbank = [Res("bank%d" % i, excl=True) for i in range(8)]
            self.cm = sb("cm", [128, 4, 128], BF16)
            self.epst = sb("epst", [128, 1], F32)
            self.dqk_t = sb("dqk_t", [128, 2 * self.nl], F32)
            self.Rconst = Res("const")
            ph = Phase(ctx, "setup")
            ph.dma("sync", self.cm[:], self.cmat.rearrange("a p n -> p a n"), self.Rconst, pwrites=[self.Rconst])
            ph.dma("sync", self.dqk_t[:], self.dqk, self.Rconst, pwrites=[self.Rconst])
            ph.op("vector", lambda e: e.memset(self.epst[:], EPS), pwrites=[self.Rconst])
            ph.emit()
            self.ident = self.cm[:, 0, :]
            self.ones = self.cm[:, 1, :]
            self.R1 = self.cm[:, 2, :]
            self.R2 = self.cm[:, 3, :]

            for l in range(self.nl):
                last = self.last[l]
                if self.mode == "fused":
                    x_in = self.x if l == 0 else self.xmid
                    x_out = self.out if last else self.xmid
                    self.kv_loc = self.kv_locs[l]
                    self.kv_all = self.kv_alls[l]
                else:
                    x_in = self.x
                    x_out = getattr(self, "out", None)
                with ExitStack() as ls:
                    hT = ls.enter_context(nc.sbuf_tensor("hT%d" % l, [128, KC, T], BF16))
                    RhT = [Res("hT%d" % t) for t in range(4)]
                    self.phase0(l, x_in, hT, RhT)
                    jobs = {"kv": KV_JOBS, "main": MAIN_JOBS, "fused": KV_JOBS + MAIN_JOBS}[self.mode]
                    self.phaseA(l, jobs, hT, RhT)
                if self.mode == "kv":
                    continue
                if self.mode == "fused":
                    self.gather()
                self.phaseB(l)
                self.phaseC(l, x_in, x_out, last)
        return nc

    def phase0(self, l, x_in, hT, RhT):
        nc, ctx = self.nc, self.ctx
        ph = Phase(ctx, "p0")
        with ExitStack() as st:
            sb = lambda name, shape, dt: st.enter_context(nc.sbuf_tensor(name, shape, dt))
            xt = [sb("p0xt%d" % i, [128, D], F32) for i in range(2)]
            hn = [sb("p0hn%d" % i, [128, D], BF16) for i in range(2)]
            junk = sb("p0junk", [128, D], BF16)
            gb = sb("p0gb", [128, D], F32)
            ss = [sb("p0ss%d" % i, [128, 1], F32) for i in range(2)]
            Rxt = [Res() for _ in range(2)]
            Rhn = [Res() for _ in range(2)]
            Rss = [Res() for _ in range(2)]
            Rgb = Res()
            ph.dma("sync", gb[:], self.norm_g[l:l + 1, :].partition_broadcast(128), Rgb, writes=[Rgb])
            ev = 0
            for i in range(16):
                b = i % 2
                ph.dma("sync", xt[b][:], x_in[i * 128:(i + 1) * 128, :], Rxt[b], writes=[Rxt[b]])
                ph.op("scalar", lambda e, b=b: e.activation(out=junk[:], in_=xt[b][:], func=AF.Square,
                                                         accum_out=ss[b][:]),
                      reads=[Rxt[b]], writes=[Rss[b]])
                ph.op("scalar", lambda e, b=b: e.activation(out=ss[b][:], in_=ss[b][:], func=AF.Sqrt,
                                                         scale=1.0 / D, bias=self.epst[:, 0:1]),
                      reads=[self.Rconst], writes=[Rss[b]])
                ph.op("vector", lambda e, b=b: e.reciprocal(out=ss[b][:], in_=ss[b][:]), writes=[Rss[b]])
                ph.op("vector", lambda e, b=b: e.scalar_tensor_tensor(
                    out=hn[b][:], in0=xt[b][:], scalar=ss[b][:, 0:1], in1=gb[:], op0=ALU.mult, op1=ALU.mult),
                    reads=[Rxt[b], Rss[b], Rgb], writes=[Rhn[b]])
                for cg in range(4):
                    bk = cg % 4
                    bview = self.banks[bk][:].bitcast(BF16)
                    ph.op("tensor", [
                        (lambda e, c=cg * 4 + k, k=k, b=b, bview=bview: e.transpose(
                            out=bview[:, k * 128:(k + 1) * 128], in_=hn[b][:, c * 128:(c + 1) * 128],
                            identity=self.ident)) for k in range(4)],
                        reads=[Rhn[b], self.Rconst], writes=[self.Rbank[bk]])
                    src = bview[:, 0:512].rearrange("p (a n) -> p a n", a=4)
                    dst = hT[:, cg * 4:(cg + 1) * 4, i * 128:(i + 1) * 128]
                    if ev % 2 == 0:
                        ph.op("vector", lambda e, s=src, d=dst: e.tensor_copy(out=d, in_=s),
                              reads=[self.Rbank[bk]], pwrites=[RhT[i // 4]])
                    else:
                        ph.op("scalar", lambda e, s=src, d=dst: e.copy(out=d, in_=s),
                              reads=[self.Rbank[bk]], pwrites=[RhT[i // 4]])
                    ev += 1
            ph.emit()

    def phaseA(self, l, jobs, hT, RhT):
        nc, ctx = self.nc, self.ctx
        ph = Phase(ctx, "pA")
        w_l = self.w_in[l].rearrange("(k p) n -> p k n", p=128)
        with ExitStack() as st:
            sb = lambda name, shape, dt: st.enter_context(nc.sbuf_tensor(name, shape, dt))
            NW = 3
            wb = [sb("pAw%d" % i, [128, KC, 512], BF16) for i in range(NW)]
            Rwb = [Res() for _ in range(NW)]
            ropeb = sb("pArope", [128, 2, T], F32)
            Rrope = Res()
            gbt = sb("pAgb", [128, 64], F32)
            Rgbt = Res()
            xb = [sb("pAxb%d" % i, [128, 512], BF16) for i in range(2)]
            Rxb = [Res() for _ in range(2)]
            sq = [sb("pAsq%d" % i, [128, 512], BF16) for i in range(2)]
            Rsq = [Res() for _ in range(2)]
            t1 = [sb("pAt1%d" % i, [128, 512], F32) for i in range(2)]
            Rt1 = [Res() for _ in range(2)]
            t2 = [sb("pAt2%d" % i, [128, 512], F32) for i in range(2)]
            Rt2 = [Res() for _ in range(2)]
            rr = [sb("pArr%d" % i, [128, 512], F32) for i in range(2)]
            Rrr = [Res() for _ in range(2)]
            xn = [sb("pAxn%d" % i, [128, 512], F32) for i in range(2)]
            Rxn = [Res() for _ in range(2)]
            stg = [sb("pAstg%d" % i, [128, 512], BF16) for i in range(4)]
            Rstg = [Res() for _ in range(4)]
            if self.mode in ("main", "fused"):
                ph.dma("sync", gbt[:], self.gbT[l], Rgbt, writes=[Rgbt])

            cnt = dict(bank=0, aux=0, xb=0, sq=0, t=0, rr=0, xn=0, stg=0, rope=None)

            def load_w(ji):
                c0, ncols = self.colmap(jobs[ji][0]), jobs[ji][1]
                s = ji % NW
                ph.dma("gpsimd", wb[s][:, :, 0:ncols], w_l[:, :, c0:c0 + ncols], Rwb[s], writes=[Rwb[s]])

            def need_rope(tb):
                if cnt["rope"] != tb:
                    cnt["rope"] = tb
                    ph.dma("sync", ropeb[:, 0, :], self.ropeT[tb], Rrope, writes=[Rrope])
                    ph.dma("sync", ropeb[:, 1, :], self.ropeT[tb + 1], Rrope, pwrites=[Rrope])

            def dest_fm(kind, idx, t):
                ts = slice(t * 512, (t + 1) * 512)
                if kind in ("rope_q", "dnorm_q", "scale_q"):
                    return self.qT_s[idx][:, ts]
                if kind in ("rope_k", "dnorm_k", "plain_k"):
                    return self.kv_loc[idx * 128:(idx + 1) * 128, ts]
                if kind == "silu":
                    return self.zs_s[idx][:, ts]
                if kind == "gate":
                    return self.gs_s[idx][:, ts]
                raise ValueError(kind)

            def rot(key, n):
                v = cnt[key]
                cnt[key] = v + 1
                return v % n

            def rope_tail(bank, Rb, src_ap, Rsrc_list, xb_i, ts, Rmat, dst):
                ba = 4 + rot("aux", 2)
                ph.op("tensor", lambda e, ba=ba, xb_i=xb_i: e.matmul(self.banks[ba][:], Rmat, xb[xb_i][:],
                                                                      start=True, stop=True),
                      reads=[Rxb[xb_i], self.Rconst], writes=[self.Rbank[ba]])
                ti = rot("t", 2)
                ph.op("vector", lambda e, ti=ti: e.tensor_tensor(out=t1[ti][:], in0=src_ap, in1=ropeb[:, 0, ts],
                                                               op=ALU.mult),
                      reads=Rsrc_list + [Rrope], writes=[Rt1[ti]])
                ph.op("vector", lambda e, ti=ti, ba=ba: e.tensor_tensor(out=t2[ti][:], in0=self.banks[ba][:],
                                                                      in1=ropeb[:, 1, ts], op=ALU.mult),
                      reads=[self.Rbank[ba], Rrope], writes=[Rt2[ti]])
                si = rot("stg", 4)
                ph.op("vector", lambda e, ti=ti, si=si: e.tensor_tensor(out=stg[si][:], in0=t1[ti][:],
                                                                      in1=t2[ti][:], op=ALU.add),
                      reads=[Rt1[ti], Rt2[ti]], writes=[Rstg[si]])
                ph.dma("sync", dst, stg[si][:], Rstg[si], reads=[Rstg[si]])

            if len(jobs) > 0:
                load_w(0)
            if len(jobs) > 1:
                load_w(1)
            for ji, (c0, ncols, kind, d0) in enumerate(jobs):
                if ji + 2 < len(jobs):
                    load_w(ji + 2)
                s = ji % NW
                if kind == "v":
                    nh = ncols // 128
                    for sub in range(16):
                        bk = rot("bank", 4)
                        ph.op("tensor", [
                            (lambda e, k=k, bk=bk, sub=sub, s=s, ncols=ncols: e.matmul(
                                self.banks[bk][:, 0:ncols], hT[:, k, sub * 128:(sub + 1) * 128],
                                wb[s][:, k, 0:ncols], start=(k == 0), stop=(k == KC - 1))) for k in range(KC)],
                            reads=[Rwb[s], RhT[sub // 4]], writes=[self.Rbank[bk]])
                        si = rot("stg", 4)
                        ph.op("scalar", lambda e, bk=bk, si=si, ncols=ncols: e.copy(
                            out=stg[si][:, 0:ncols], in_=self.banks[bk][:, 0:ncols]),
                              reads=[self.Rbank[bk]], writes=[Rstg[si]])
                        vrow = 1536 + d0 * 128
                        dst = self.kv_loc[vrow:vrow + nh * 128, :].rearrange(
                            "(s a) (b d) -> s (a b) d", s=nh, d=128)[:, sub * 128:(sub + 1) * 128, :].rearrange(
                            "s t d -> t s d")
                        ph.dma("scalar", dst, stg[si][:, 0:ncols].rearrange("p (s d) -> p s d", d=128), Rstg[si],
                               reads=[Rstg[si]])
                    continue
                if kind in ("rope_q", "rope_k"):
                    need_rope(0 if kind == "rope_q" else 2)
                if kind in ("dnorm_q", "dnorm_k"):
                    need_rope(4 if kind == "dnorm_q" else 6)
                for f in range(ncols // 128):
                    for t in range(4):
                        ts = slice(t * 512, (t + 1) * 512)
                        bk = rot("bank", 4)
                        bank = self.banks[bk]
                        Rb = self.Rbank[bk]
                        ph.op("tensor", [
                            (lambda e, k=k, bk=bk, f=f, ts=ts, s=s: e.matmul(
                                self.banks[bk][:], wb[s][:, k, f * 128:(f + 1) * 128], hT[:, k, ts],
                                start=(k == 0), stop=(k == KC - 1))) for k in range(KC)],
                            reads=[Rwb[s], RhT[t]], writes=[Rb])
                        dst = dest_fm(kind, d0 + f, t)
                        if kind in ("rope_q", "rope_k"):
                            xi = rot("xb", 2)
                            ph.op("scalar", lambda e, bk=bk, xi=xi: e.copy(out=xb[xi][:], in_=self.banks[bk][:]),
                                  reads=[Rb], writes=[Rxb[xi]])
                            rope_tail(bank, Rb, bank[:], [Rb], xi, ts, self.R1, dst)
                        elif kind in ("dnorm_q", "dnorm_k"):
                            gcol = self.dqk_t[:, 2 * l + (0 if kind == "dnorm_q" else 1):
                                              2 * l + (0 if kind == "dnorm_q" else 1) + 1]
                            qi = rot("sq", 2)
                            ph.op("scalar", lambda e, bk=bk, qi=qi: e.activation(out=sq[qi][:],
                                                                               in_=self.banks[bk][:],
                                                                               func=AF.Square),
                                  reads=[Rb], writes=[Rsq[qi]])
                            ba = 4 + rot("aux", 2)
                            ph.op("tensor", lambda e, ba=ba, qi=qi: e.matmul(self.banks[ba][:], self.ones,
                                                                            sq[qi][:], start=True, stop=True),
                                  reads=[Rsq[qi], self.Rconst], writes=[self.Rbank[ba]])
                            ri = rot("rr", 2)
                            ph.op("scalar", lambda e, ba=ba, ri=ri: e.activation(
                                out=rr[ri][:], in_=self.banks[ba][:], func=AF.Sqrt, scale=1.0 / 128,
                                bias=self.epst[:, 0:1]),
                                reads=[self.Rbank[ba], self.Rconst], writes=[Rrr[ri]])
                            ph.op("vector", lambda e, ri=ri: e.reciprocal(out=rr[ri][:], in_=rr[ri][:]),
                                  writes=[Rrr[ri]])
                            ni = rot("xn", 2)
                            ph.op("vector", lambda e, bk=bk, ri=ri, ni=ni, gcol=gcol: e.scalar_tensor_tensor(
                                out=xn[ni][:], in0=self.banks[bk][:], scalar=gcol, in1=rr[ri][:],
                                op0=ALU.mult, op1=ALU.mult),
                                reads=[Rb, Rrr[ri], self.Rconst], writes=[Rxn[ni]])
                            xi = rot("xb", 2)
                            ph.op("scalar", lambda e, ni=ni, xi=xi: e.copy(out=xb[xi][:], in_=xn[ni][:]),
                                  reads=[Rxn[ni]], writes=[Rxb[xi]])
                            rope_tail(bank, Rb, xn[ni][:], [Rxn[ni]], xi, ts, self.R2, dst)
                        else:
                            si = rot("stg", 4)
                            if kind == "scale_q":
                                fn = lambda e, bk=bk, si=si: e.activation(out=stg[si][:], in_=self.banks[bk][:],
                                                                          func=AF.Copy, scale=SCALE)
                                rd = [Rb]
                            elif kind == "plain_k":
                                fn = lambda e, bk=bk, si=si: e.copy(out=stg[si][:], in_=self.banks[bk][:])
                                rd = [Rb]
                            elif kind == "silu":
                                fn = lambda e, bk=bk, si=si: e.activation(out=stg[si][:], in_=self.banks[bk][:],
                                                                          func=AF.Silu)
                                rd = [Rb]
                            else:
                                gi = d0 + f
                                fn = lambda e, bk=bk, si=si, gi=gi: e.activation(
                                    out=stg[si][:], in_=self.banks[bk][:], func=AF.Sigmoid,
                                    bias=gbt[:, gi:gi + 1])
                                rd = [Rb, Rgbt]
                            ph.op("scalar", fn, reads=rd, writes=[Rstg[si]])
                            ph.dma("scalar", dst, stg[si][:], Rstg[si], reads=[Rstg[si]])
            ph.emit()

    def gather(self):
        ctx = self.ctx
        ph = Phase(ctx, "ag")
        ph.op("gpsimd", lambda e: e.collective_compute(
            "AllGather", ALU.bypass, replica_groups=[list(range(NCORE))],
            ins=[self.kv_loc.opt()], outs=[self.kv_all.opt()]))
        ph.emit()

    def kv_base(self):
        if self.mode == "main":
            return self.kv_all
        return self.kv_all.rearrange("(r n) t -> r n t", r=NCORE)

    def phaseB(self, l):
        nc, ctx = self.nc, self.ctx
        ph = Phase(ctx, "pB")
        with ExitStack() as st:
            sb = lambda name, shape, dt: st.enter_context(nc.sbuf_tensor(name, shape, dt))
            kD = sb("pBkD", [128, NCORE, T], BF16)
            vD = sb("pBvD", [128, NCORE, 16, 128], BF16)
            RkD = [Res() for _ in range(NCORE)]
            RvD = [Res() for _ in range(NCORE)]
            hk = [sb("pBhk%d" % i, [128, 4096], BF16) for i in range(2)]
            hv = [sb("pBhv%d" % i, [128, 32, 128], BF16) for i in range(2)]
            Rhk = [Res() for _ in range(2)]
            Rhv = [Res() for _ in range(2)]
            qb = [sb("pBq%d" % i, [128, 3, 512], BF16) for i in range(2)]
            Rqb = [Res() for _ in range(2)]
            NP = 4
            pt = [sb("pBpt%d" % i, [128, 512], BF16) for i in range(NP)]
            Rpt = [Res() for _ in range(NP)]
            mk = [sb("pBmk%d" % i, [128, 512], BF16) for i in range(NP)]
            Rmk = [Res() for _ in range(NP)]
            bs = [sb("pBbs%d" % i, [128, 512], F32) for i in range(2)]
            Rbs = [Res() for _ in range(2)]
            dn = [sb("pBdn%d" % i, [128, 512], F32) for i in range(2)]
            Rdn = [Res() for _ in range(2)]
            yy = [sb("pByy%d" % i, [128, 512], F32) for i in range(2)]
            Ryy = [Res() for _ in range(2)]
            zt = [sb("pBzt%d" % i, [128, 512], BF16) for i in range(2)]
            Rzt = [Res() for _ in range(2)]
            yo = [sb("pByo%d" % i, [128, 512], BF16) for i in range(2)]
            Ryo = [Res() for _ in range(2)]
            esk = sb("pBesk", [128, 4], F32)
            Resk = Res()
            ph.dma("sync", esk[:], self.asink[0:1, 4 * l:4 * l + 4].partition_broadcast(128), Resk, writes=[Resk])
            ph.op("scalar", lambda e: e.activation(out=esk[:], in_=esk[:], func=AF.Exp), writes=[Resk])

            cnt = dict(q=0, halo=0, unit=0, mk=0, bs=0, ep=0)
            steps = []

            dyn = {}
            base3 = self.kv_base()

            def left(e):
                if "left" not in dyn:
                    dyn["pid"] = e.partition_id()
                    dyn["left"] = (dyn["pid"] + (NCORE - 1)) % NCORE
                    dyn["right"] = (dyn["pid"] + 1) % NCORE
                return dyn["left"]

            def right(e):
                left(e)
                return dyn["right"]

            Rh = Res()
            ph.dma("sync", self.kvL_K, lambda e: base3[bass.ds(left(e), 1), 0:1280, :], Rh, pwrites=[Rh])
            ph.dma("sync", self.kvL_V, lambda e: base3[bass.ds(left(e), 1), 1536:2816, :], Rh, pwrites=[Rh])
            ph.dma("sync", self.kvR_K, lambda e: base3[bass.ds(right(e), 1), 0:1280, :], Rh, pwrites=[Rh])
            ph.dma("sync", self.kvR_V, lambda e: base3[bass.ds(right(e), 1), 1536:2816, :], Rh, pwrites=[Rh])

            def tokview(ap2d, tok0, ntok):
                return ap2d.rearrange("a (b d) -> (a b) d", d=128)[tok0:tok0 + ntok, :].rearrange(
                    "(j p) d -> p j d", p=128)

            def load_halo(hs, kvs, H):
                krows = slice(kvs * 128, (kvs + 1) * 128)
                vrows = slice(1536 + kvs * 128, 1536 + (kvs + 1) * 128)
                nh = H // 128
                vr = slice(kvs * 128, (kvs + 1) * 128)
                ph.dma("sync", hk[hs][:, 0:H], self.kvL_K[krows, T - H:T], Rhk[hs],
                       reads=[Rh], writes=[Rhk[hs]])
                ph.dma("sync", hk[hs][:, H:H + T], self.kv_loc[krows, :], Rhk[hs], pwrites=[Rhk[hs]])
                ph.dma("sync", hk[hs][:, H + T:H + T + H], self.kvR_K[krows, 0:H], Rhk[hs],
                       reads=[Rh], pwrites=[Rhk[hs]])
                ph.dma("sync", hv[hs][:, 0:nh, :], tokview(self.kvL_V[vr, :], T - H, H), Rhv[hs],
                       reads=[Rh], writes=[Rhv[hs]])
                ph.dma("sync", hv[hs][:, nh:nh + 16, :], tokview(self.kv_loc[vrows, :], 0, T),
                       Rhv[hs], pwrites=[Rhv[hs]])
                ph.dma("sync", hv[hs][:, nh + 16:nh + 16 + nh, :], tokview(self.kvR_V[vr, :], 0, H),
                       Rhv[hs], reads=[Rh], pwrites=[Rhv[hs]])

            def load_d(kvi, r):
                kvs = 10 + kvi
                krows = slice(kvs * 128, (kvs + 1) * 128)
                vrows = slice(1536 + kvs * 128, 1536 + (kvs + 1) * 128)
                ph.dma("sync", kD[:, r, :], base3[r, krows, :], RkD[r], writes=[RkD[r]])
                ph.dma("sync", vD[:, r, :, :], tokview(base3[r, vrows, :], 0, T), RvD[r], writes=[RvD[r]])

            for br in BRANCHES:
                H = br["halo"]
                nq = len(br["groups"])
                if br["name"] == "C":
                    for r in range(NCORE):
                        steps.append(("loadD", 0, r))
                for kvi, kvs in enumerate(br["kvslots"]):
                    hs = cnt["halo"] % 2
                    cnt["halo"] += 1
                    steps.append(("halo", hs, kvs, H))
                    for t in range(4):
                        ts = slice(t * 512, (t + 1) * 512)
                        for gi in range(br["hq_per_kv"]):
                            hq = kvi * br["hq_per_kv"] + gi
                            qs = cnt["q"] % 2
                            cnt["q"] += 1
                            q0 = br["qbase"] + hq
                            qsrc = self.qT_s[q0:q0 + br["qstride"] * (nq - 1) + 1:br["qstride"], :, ts].rearrange(
                                "g p n -> p g n")
                            steps.append(("q", qs, nq, qsrc))
                            u = cnt["unit"]
                            cnt["unit"] += 1
                            ob, db = 3 + (u % 2), 5 + (u % 2)
                            tiles = []
                            for g, (mbase, ntile, k0off) in enumerate(br["groups"]):
                                for j in range(ntile):
                                    col = t * 512 + k0off + 128 * j + H
                                    tiles.append(dict(
                                        q=qb[qs][:, g, :], Rq=Rqb[qs],
                                        kT=hk[hs][:, col:col + 128], Rk=Rhk[hs],
                                        v=hv[hs][:, col // 128, :], Rv=Rhv[hs],
                                        mask=(None if mbase is None else (t, mbase + j)),
                                        bias=((BVAR[t], hq, j) if br["bias"] else None)))
                            for i, tl in enumerate(tiles):
                                tl.update(ob=ob, db=db, first=(i == 0), last=(i == len(tiles) - 1),
                                          ep=(dict(n=br["n"], hq=hq, t=t, sink=br["sink"])
                                              if i == len(tiles) - 1 else None))
                            steps.extend(tiles)

            for kvi in range(2):
                for t in range(4):
                    ts = slice(t * 512, (t + 1) * 512)
                    for gi in range(2):
                        hq = kvi * 2 + gi
                        qs = cnt["q"] % 2
                        cnt["q"] += 1
                        steps.append(("q", qs, 1, self.qT_s[20 + hq:21 + hq, :, ts].rearrange("g p n -> p g n")))
                        u = cnt["unit"]
                        cnt["unit"] += 1
                        ob, db = 3 + (u % 2), 5 + (u % 2)
                        n = NCORE * 16
                        for i in range(n):
                            r, j = divmod(i, 16)
                            steps.append(dict(
                                q=qb[qs][:, 0, :], Rq=Rqb[qs],
                                kT=kD[:, r, j * 128:(j + 1) * 128], Rk=RkD[r],
                                v=vD[:, r, j, :], Rv=RvD[r], mask=None, bias=None,
                                ob=ob, db=db, first=(i == 0), last=(i == n - 1),
                                ep=(dict(n=3, hq=hq, t=t, sink=False) if i == n - 1 else None)))
                            if kvi == 0 and t == 3 and gi == 1 and j == 15:
                                steps.append(("loadD", 1, r))

            LAG = 2
            pend = []
            sc = dict(s=0, p=0)

            def emit_front(tl):
                sbk = sc["s"] % 3
                sc["s"] += 1
                pi = sc["p"] % NP
                sc["p"] += 1
                tl["pi"] = pi
                ph.op("tensor", lambda e, sbk=sbk, tl=tl: e.matmul(self.banks[sbk][:], tl["kT"], tl["q"],
                                                                  start=True, stop=True),
                      reads=[tl["Rk"], tl["Rq"]], writes=[self.Rbank[sbk]])
                ph.op("scalar", lambda e, sbk=sbk, pi=pi: e.activation(out=pt[pi][:], in_=self.banks[sbk][:],
                                                                     func=AF.Exp),
                      reads=[self.Rbank[sbk]], writes=[Rpt[pi]])
                if tl["mask"] is not None:
                    mi = cnt["mk"] % NP
                    cnt["mk"] += 1
                    tv, midx = tl["mask"]
                    ph.dma("sync", mk[mi][:], self.maskAC[tv, midx], Rmk[mi], writes=[Rmk[mi]])
                    ph.op("vector", lambda e, pi=pi, mi=mi: e.tensor_tensor(out=pt[pi][:], in0=pt[pi][:],
                                                                          in1=mk[mi][:], op=ALU.mult),
                          reads=[Rmk[mi]], writes=[Rpt[pi]])
                if tl["bias"] is not None:
                    bi = cnt["bs"] % 2
                    cnt["bs"] += 1
                    mi = cnt["mk"] % NP
                    cnt["mk"] += 1
                    v, h, j = tl["bias"]
                    ph.dma("sync", bs[bi][:], self.biasB[l, v, h, j], Rbs[bi], writes=[Rbs[bi]])
                    ph.op("scalar", lambda e, bi=bi, mi=mi: e.activation(out=mk[mi][:], in_=bs[bi][:],
                                                                       func=AF.Exp),
                          reads=[Rbs[bi]], writes=[Rmk[mi]])
                    ph.op("vector", lambda e, pi=pi, mi=mi: e.tensor_tensor(out=pt[pi][:], in0=pt[pi][:],
                                                                          in1=mk[mi][:], op=ALU.mult),
                          reads=[Rmk[mi]], writes=[Rpt[pi]])

            def emit_back(tl):
                pi, ob, db = tl["pi"], tl["ob"], tl["db"]
                kw = dict(writes=[self.Rbank[ob], self.Rbank[db]]) if tl["first"] else \
                    dict(pwrites=[self.Rbank[ob], self.Rbank[db]])
                ph.op("tensor", [
                    lambda e, tl=tl, pi=pi, ob=ob: e.matmul(self.banks[ob][:], tl["v"], pt[pi][:],
                                                           start=tl["first"], stop=tl["last"]),
                    lambda e, tl=tl, pi=pi, db=db: e.matmul(self.banks[db][:], self.ones, pt[pi][:],
                                                           start=tl["first"], stop=tl["last"])],
                    reads=[Rpt[pi], tl["Rv"], self.Rconst], **kw)
                if tl["ep"] is not None:
                    ep = tl["ep"]
                    ei = cnt["ep"] % 2
                    cnt["ep"] += 1
                    zc = ep["n"] * 4 + ep["hq"]
                    ts = slice(ep["t"] * 512, (ep["t"] + 1) * 512)
                    ph.dma("sync", zt[ei][:], self.zs_s[zc][:, ts], Rzt[ei], writes=[Rzt[ei]])
                    if ep["sink"]:
                        hq = ep["hq"]
                        ph.op("vector", lambda e, ei=ei, db=db, hq=hq: e.tensor_scalar(
                            out=dn[ei][:], in0=self.banks[db][:], scalar1=esk[:, hq:hq + 1], scalar2=None,
                            op0=ALU.add),
                            reads=[self.Rbank[db], Resk], writes=[Rdn[ei]])
                        ph.op("vector", lambda e, ei=ei: e.reciprocal(out=dn[ei][:], in_=dn[ei][:]),
                              writes=[Rdn[ei]])
                    else:
                        ph.op("vector", lambda e, ei=ei, db=db: e.reciprocal(out=dn[ei][:], in_=self.banks[db][:]),
                              reads=[self.Rbank[db]], writes=[Rdn[ei]])
                    ph.op("vector", lambda e, ei=ei, ob=ob: e.tensor_tensor(out=yy[ei][:], in0=self.banks[ob][:],
                                                                          in1=dn[ei][:], op=ALU.mult),
                          reads=[self.Rbank[ob], Rdn[ei]], writes=[Ryy[ei]])
                    ph.op("vector", lambda e, ei=ei: e.tensor_tensor(out=yo[ei][:], in0=yy[ei][:], in1=zt[ei][:],
                                                                   op=ALU.mult),
                          reads=[Ryy[ei], Rzt[ei]], writes=[Ryo[ei]])
                    ph.dma("gpsimd", self.ysz_s[zc][:, ts], yo[ei][:], Ryo[ei], reads=[Ryo[ei]])

            for item in steps:
                if isinstance(item, tuple):
                    if item[0] == "q":
                        _, qs, nq, qsrc = item
                        ph.dma("sync", qb[qs][:, 0:nq, :], qsrc, Rqb[qs], writes=[Rqb[qs]])
                    else:
                        while pend:
                            emit_back(pend.pop(0))
                        if item[0] == "halo":
                            load_halo(item[1], item[2], item[3])
                        else:
                            load_d(item[1], item[2])
                    continue
                emit_front(item)
                pend.append(item)
                if len(pend) > LAG:
                    emit_back(pend.pop(0))
            while pend:
                emit_back(pend.pop(0))
            ph.emit()

    def phaseC(self, l, x_in, x_out, last):
        nc, ctx = self.nc, self.ctx
        ph = Phase(ctx, "pC")
        wbr = self.w_br[l].rearrange("(k p) n -> p k n", p=128)
        wo = self.w_out[l].rearrange("(k p) n -> p k n", p=128)
        with ExitStack() as st:
            sb = lambda name, shape, dt: st.enter_context(nc.sbuf_tensor(name, shape, dt))
            NW = 3
            wb = [sb("pCw%d" % i, [128, KC, 512], BF16) for i in range(NW)]
            Rwb = [Res() for _ in range(NW)]
            ysz = [sb("pCy%d" % i, [128, 16, 512], BF16) for i in range(2)]
            Rysz = [Res() for _ in range(2)]
            gt = [sb("pCg%d" % i, [128, 512], BF16) for i in range(4)]
            Rgt = [Res() for _ in range(4)]
            acc = [sb("pCa%d" % i, [128, 512], F32) for i in range(2)]
            Racc = [Res() for _ in range(2)]
            tmp = [sb("pCt%d" % i, [128, 512], F32) for i in range(2)]
            Rtmp = [Res() for _ in range(2)]
            mT = [sb("pCm%d" % i, [128, 16, 512], BF16) for i in range(2)]
            RmT = [Res() for _ in range(2)]
            xf = sb("pCxf", [128, 4, D], F32)
            Rxf = [Res() for _ in range(4)]
            if last:
                gfb = sb("pCgf", [128, D], F32)
                Rgf = Res()
                ss = [sb("pCss%d" % i, [128, 1], F32) for i in range(2)]
                Rss = [Res() for _ in range(2)]
                junk = sb("pCjunk", [128, D], BF16)
                ph.dma("sync", gfb[:], self.fng[0:1, :].partition_broadcast(128), Rgf, writes=[Rgf])

            wjobs = []
            for t in range(4):
                wjobs += [("br", jb) for jb in range(4)] + [("out", fo) for fo in range(4)]
            cnt = dict(bank=0, g=0, a=0, tmp=0)

            def load_w(wi):
                kind, b = wjobs[wi]
                src = (wbr if kind == "br" else wo)[:, :, b * 512:(b + 1) * 512]
                s = wi % NW
                ph.dma("gpsimd", wb[s][:], src, Rwb[s], writes=[Rwb[s]])

            load_w(0)
            load_w(1)
            wi = 0
            for t in range(4):
                ts = slice(t * 512, (t + 1) * 512)
                yb = t % 2
                for q4 in range(4):
                    ph.dma("sync", ysz[yb][:, 4 * q4:4 * q4 + 4, :],
                           self.ysz_s[4 * q4:4 * q4 + 4, :, ts].rearrange("c p n -> p c n"), Rysz[yb],
                           **(dict(writes=[Rysz[yb]]) if q4 == 0 else dict(pwrites=[Rysz[yb]])))
                for s4 in range(4):
                    r0 = t * 512 + s4 * 128
                    ph.dma("sync", xf[:, s4, :], x_in[r0:r0 + 128, :], Rxf[s4], writes=[Rxf[s4]])
                mb = t % 2
                for jb in range(4):
                    if wi + 2 < len(wjobs):
                        load_w(wi + 2)
                    s = wi % NW
                    wi += 1
                    for jj in range(4):
                        j = jb * 4 + jj
                        ai = cnt["a"] % 2
                        cnt["a"] += 1
                        for n in range(4):
                            bk = cnt["bank"] % 4
                            cnt["bank"] += 1
                            ph.op("tensor", [
                                (lambda e, m=m, n=n, bk=bk, s=s, jj=jj, yb=yb: e.matmul(
                                    self.banks[bk][:], wb[s][:, n * 4 + m, jj * 128:(jj + 1) * 128],
                                    ysz[yb][:, n * 4 + m, :], start=(m == 0), stop=(m == 3))) for m in range(4)],
                                reads=[Rwb[s], Rysz[yb]], writes=[self.Rbank[bk]])
                            gi = cnt["g"] % 4
                            cnt["g"] += 1
                            ph.dma("sync", gt[gi][:], self.gs_s[n * 16 + j][:, ts], Rgt[gi], writes=[Rgt[gi]])
                            if n == 0:
                                ph.op("vector", lambda e, bk=bk, gi=gi, ai=ai: e.tensor_tensor(
                                    out=acc[ai][:], in0=self.banks[bk][:], in1=gt[gi][:], op=ALU.mult),
                                    reads=[self.Rbank[bk], Rgt[gi]], writes=[Racc[ai]])
                            else:
                                ti = cnt["tmp"] % 2
                                cnt["tmp"] += 1
                                ph.op("vector", lambda e, bk=bk, gi=gi, ti=ti: e.tensor_tensor(
                                    out=tmp[ti][:], in0=self.banks[bk][:], in1=gt[gi][:], op=ALU.mult),
                                    reads=[self.Rbank[bk], Rgt[gi]], writes=[Rtmp[ti]])
                                if n < 3:
                                    ph.op("vector", lambda e, ai=ai, ti=ti: e.tensor_tensor(
                                        out=acc[ai][:], in0=acc[ai][:], in1=tmp[ti][:], op=ALU.add),
                                        reads=[Rtmp[ti]], writes=[Racc[ai]])
                                else:
                                    ph.op("vector", lambda e, ai=ai, ti=ti, j=j, mb=mb: e.tensor_tensor(
                                        out=mT[mb][:, j, :], in0=acc[ai][:], in1=tmp[ti][:], op=ALU.add),
                                        reads=[Rtmp[ti], Racc[ai]],
                                        **(dict(writes=[RmT[mb]]) if j == 0 else dict(pwrites=[RmT[mb]])))
                for fo in range(4):
                    if wi + 2 < len(wjobs):
                        load_w(wi + 2)
                    s = wi % NW
                    wi += 1
                    fs = slice(fo * 512, (fo + 1) * 512)
                    for s4 in range(4):
                        bk = cnt["bank"] % 4
                        cnt["bank"] += 1
                        ph.op("tensor", [
                            (lambda e, j=j, bk=bk, s=s, s4=s4, mb=mb: e.matmul(
                                self.banks[bk][:], mT[mb][:, j, s4 * 128:(s4 + 1) * 128], wb[s][:, j, :],
                                start=(j == 0), stop=(j == KC - 1))) for j in range(KC)],
                            reads=[Rwb[s], RmT[mb]], writes=[self.Rbank[bk]])
                        ph.op("vector", lambda e, bk=bk, s4=s4, fs=fs: e.tensor_tensor(
                            out=xf[:, s4, fs], in0=self.banks[bk][:], in1=xf[:, s4, fs], op=ALU.add),
                            reads=[self.Rbank[bk]], writes=[Rxf[s4]])
                for s4 in range(4):
                    r0 = t * 512 + s4 * 128
                    if last:
                        b = s4 % 2
                        ph.op("scalar", lambda e, s4=s4, b=b: e.activation(out=junk[:], in_=xf[:, s4, :],
                                                                         func=AF.Square, accum_out=ss[b][:]),
                              reads=[Rxf[s4]], writes=[Rss[b]])
                        ph.op("scalar", lambda e, b=b: e.activation(out=ss[b][:], in_=ss[b][:], func=AF.Sqrt,
                                                                 scale=1.0 / D, bias=self.epst[:, 0:1]),
                              reads=[self.Rconst], writes=[Rss[b]])
                        ph.op("vector", lambda e, b=b: e.reciprocal(out=ss[b][:], in_=ss[b][:]), writes=[Rss[b]])
                        ph.op("vector", lambda e, s4=s4, b=b: e.scalar_tensor_tensor(
                            out=xf[:, s4, :], in0=xf[:, s4, :], scalar=ss[b][:, 0:1], in1=gfb[:],
                            op0=ALU.mult, op1=ALU.mult),
                            reads=[Rss[b], Rgf], writes=[Rxf[s4]])
                    ph.dma("gpsimd", x_out[r0:r0 + 128, :], xf[:, s4, :], Rxf[s4], reads=[Rxf[s4]])
            ph.emit()


def _rope_tables():
    half = 64
    try:
        import jax
        import jax.numpy as jnp
        cpu = jax.devices("cpu")[0]
        with jax.default_device(cpu):
            inv1 = np.asarray(10000.0 ** (-jnp.arange(half, dtype=jnp.float32) / half))
            inv2 = np.asarray(10000.0 ** (-jnp.arange(32, dtype=jnp.float32) / 32))
            t = jnp.arange(S)
            a1 = t.astype(jnp.float32)[:, None] * jnp.asarray(inv1)[None, :]
            ar = (t // GRID_W).astype(jnp.float32)[:, None] * jnp.asarray(inv2)[None, :]
            ac = (t % GRID_W).astype(jnp.float32)[:, None] * jnp.asarray(inv2)[None, :]
            c1, s1 = np.asarray(jnp.cos(a1)), np.asarray(jnp.sin(a1))
            cr, sr = np.asarray(jnp.cos(ar)), np.asarray(jnp.sin(ar))
            cc, sc = np.asarray(jnp.cos(ac)), np.asarray(jnp.sin(ac))
    except Exception:
        inv1 = (np.float32(10000.0) ** (-np.arange(half, dtype=np.float32) / np.float32(half))).astype(np.float32)
        inv2 = (np.float32(10000.0) ** (-np.arange(32, dtype=np.float32) / np.float32(32))).astype(np.float32)
        t = np.arange(S)
        a1 = t.astype(np.float32)[:, None] * inv1[None, :]
        ar = (t // GRID_W).astype(np.float32)[:, None] * inv2[None, :]
        ac = (t % GRID_W).astype(np.float32)[:, None] * inv2[None, :]
        c1, s1 = np.cos(a1), np.sin(a1)
        cr, sr = np.cos(ar), np.sin(ar)
        cc, sc = np.cos(ac), np.sin(ac)
    cos1 = np.concatenate([c1, c1], axis=1).T.astype(np.float32)
    sin1 = np.concatenate([-s1, s1], axis=1).T.astype(np.float32)
    cosa = np.concatenate([cr, cr, cc, cc], axis=1).T.astype(np.float32)
    sina = np.concatenate([-sr, sr, -sc, sc], axis=1).T.astype(np.float32)
    sc32 = np.float32(SCALE)
    return np.stack([cos1 * sc32, sin1 * sc32, cos1, sin1, cosa * sc32, sina * sc32, cosa, sina]).astype(np.float32)


def _cmat():
    ident = np.eye(128, dtype=np.float32)
    ones = np.ones((128, 128), np.float32)
    r1 = np.zeros((128, 128), np.float32)
    r2 = np.zeros((128, 128), np.float32)
    for m in range(128):
        r1[(m + 64) % 128, m] = 1.0
        hh, rr = divmod(m, 64)
        r2[hh * 64 + (rr + 32) % 64, m] = 1.0
    return np.stack([ident, ones, r1, r2]).astype(NPBF)


def _masks(c):
    out = np.zeros((4, 40, 128, 512), np.float32)
    p = np.arange(128)[:, None]
    f = np.arange(512)[None, :]
    specs = [(0, 6, -128, 128, 1), (6, 6, -128, 64, 1), (12, 8, -256, 256, 4), (20, 20, -1024, 1024, 16)]
    for t in range(4):
        qg = c * T + 512 * t + f
        for base, ntile, k0off, hw, dil in specs:
            for j in range(ntile):
                kg = c * T + 512 * t + k0off + 128 * j + p
                d = kg - qg
                ok = (np.abs(d) <= hw) & (d % dil == 0) & (kg >= 0) & (kg < S)
                out[t, base + j] = ok
    return out.astype(NPBF)


def _bias_b(rpb, c):
    nl = rpb.shape[0]
    out = np.full((nl, 3, 4, 8, 128, 512), NEG, np.float32)
    p = np.arange(128)
    f = np.arange(512)
    kc = (p % 64)[:, None]
    qc = (f % 64)[None, :]
    cs = np.clip(qc - 8, 0, 48)
    col_ok = (kc >= cs) & (kc < cs + 16)
    dc = np.clip(kc - qc + 15, 0, 30)
    for v, t in enumerate([0, 1, 3]):
        qr = (32 * c + 8 * t + f // 64)[None, :]
        rs = np.clip(qr - 4, 0, 256 - 8)
        for j in range(8):
            kr = (32 * c + 8 * t - 4 + 2 * j + p // 64)[:, None]
            ok = (kr >= rs) & (kr < rs + 8) & (kr >= 0) & (kr < 256) & col_ok
            dr = np.clip(kr - qr + 7, 0, 14)
            for l in range(nl):
                val = rpb[l][:, dr, dc]
                out[l, v, :, j] = np.where(ok[None], val, np.float32(NEG))
    return out


_CACHE = {}


def _program(mode, nl, last_flags):
    key = (mode, nl, tuple(last_flags))
    if key not in _CACHE:
        _CACHE[key] = Builder(mode, nl, last_flags).build()
    return _CACHE[key]


def _common_inputs(inputs, c, layers, ranges=None):
    nl = len(layers)
    d = {}
    w = inputs["w_in"][layers]
    if ranges is not None:
        w = np.concatenate([w[:, :, a:b] for a, b in ranges], axis=2)
    d["w_in"] = np.ascontiguousarray(w)
    d["norm_g"] = np.ascontiguousarray(inputs["norm_g"][layers])
    dqk = np.zeros((128, 2 * nl), np.float32)
    for i, l in enumerate(layers):
        dqk[:, 2 * i] = inputs["d_q_norm"][l]
        dqk[:, 2 * i + 1] = inputs["d_k_norm"][l]
    d["dqk"] = dqk
    return d


def _main_inputs(inputs, c, layers, masks, rope):
    nl = len(layers)
    d = {}
    d["w_br"] = np.ascontiguousarray(inputs["w_branch"][layers].reshape(nl, D, D))
    d["w_out"] = np.ascontiguousarray(inputs["w_out"][layers])
    gb = inputs["gate_b"][layers]
    d["gbT"] = np.ascontiguousarray(gb.reshape(nl, 4, 16, 128).transpose(0, 3, 1, 2).reshape(nl, 128, 64))
    d["asink"] = np.ascontiguousarray(inputs["a_sink"][layers].reshape(1, 4 * nl))
    d["fng"] = np.ascontiguousarray(inputs["final_norm_g"].reshape(1, D))
    d["maskAC"] = masks[c]
    d["biasB"] = _bias_b(np.asarray(inputs["b_rpb"])[layers], c)
    return d


FUSED = False


def kernel(**inputs):
    inputs = {k: np.asarray(v) for k, v in inputs.items()}
    x = inputs["x"].reshape(S, D).astype(np.float32, copy=False)
    rope = _rope_tables()
    cm = _cmat()
    masks = [_masks(c) for c in range(NCORE)]
    ropec = [np.ascontiguousarray(rope[:, :, c * T:(c + 1) * T]) for c in range(NCORE)]
    cores = list(range(NCORE))
    if FUSED:
        nc = _program("fused", 2, [False, True])
        maps = []
        for c in cores:
            m = {"x": np.ascontiguousarray(x[c * T:(c + 1) * T]), "ropeT": ropec[c], "cmat": cm}
            m.update(_common_inputs(inputs, c, [0, 1]))
            m.update(_main_inputs(inputs, c, [0, 1], masks, rope))
            maps.append(m)
        res = run_bass_kernel_spmd(nc, maps, core_ids=cores)
        out = np.concatenate([res.results[c]["out"] for c in cores], axis=0)
        return out.reshape(1, S, D).astype(np.float32)

    xs = [np.ascontiguousarray(x[c * T:(c + 1) * T]) for c in cores]
    for l in range(2):
        last = (l == 1)
        nc_kv = _program("kv", 1, [last])
        maps = []
        for c in cores:
            m = {"x": xs[c], "ropeT": ropec[c], "cmat": cm}
            m.update(_common_inputs(inputs, c, [l], KV_RANGES))
            maps.append(m)
        res = run_bass_kernel_spmd(nc_kv, maps, core_ids=cores)
        res_kv = [res.results[c]["kv_loc"] for c in cores]
        kv_all = np.stack(res_kv, axis=0)
        nc_main = _program("main", 1, [last])
        maps = []
        for c in cores:
            m = {"x": xs[c], "ropeT": ropec[c], "cmat": cm, "kv_all": kv_all, "kv_loc": res_kv[c]}
            m.update(_common_inputs(inputs, c, [l], MAIN_RANGES))
            m.update(_main_inputs(inputs, c, [l], masks, rope))
            maps.append(m)
        res = run_bass_kernel_spmd(nc_main, maps, core_ids=cores)
        xs = [res.results[c]["out"] for c in cores]
    out = np.concatenate(xs, axis=0)
    return out.reshape(1, S, D).astype(np.float32)
```
